# Optimizing a Trainium2 kernel written in Bass

```python
import math
import jax, jax.numpy as jnp
from jax import lax
import numpy as np

D_MODEL = 4096
BATCH = 4
SEQ = 4096
DEPTH = 2

CHUNK = 64
MIX_W = D_MODEL
W_SSM = MIX_W // 2
SSM_P = 16
SSM_G = W_SSM // SSM_P
SSM_N = 64
FOX_DH = 128
W_FOX = MIX_W - W_SSM
FOX_H = W_FOX // FOX_DH
IN_W = W_SSM + 3 * W_FOX + FOX_H
Q_BLOCK = 128
D_FF = ((8 * D_MODEL // 3 + 255) // 256) * 256
MEM_LEN = 256
X_HEADS = 4
X_DH = D_MODEL // X_HEADS
X_W = X_HEADS * X_DH
EPS = 1e-6

kernel_name = "hybrid_s5_fox_macaron_memory_block"


def rms_norm(x, g):
    xf = x.astype(jnp.float32)
    y = xf * lax.rsqrt(jnp.mean(xf * xf, axis=-1, keepdims=True) + EPS)
    return (y * g.astype(jnp.float32)).astype(x.dtype)


def swiglu(x, w_gate, w_up, w_down):
    return (jax.nn.silu(x @ w_gate) * (x @ w_up)) @ w_down


def _complex_linear_recurrence(e1, e2):
    a1r, a1i, b1r, b1i = e1
    a2r, a2i, b2r, b2i = e2
    return (a2r * a1r - a2i * a1i,
            a2r * a1i + a2i * a1r,
            a2r * b1r - a2i * b1i + b2r,
            a2r * b1i + a2i * b1r + b2i)


def s5_mixer(u, A_re, A_im, log_step, B_re, B_im, C_re, C_im, D, w_glu):
    f32 = jnp.float32
    bsz, L, _ = u.shape
    uf = u.astype(f32).reshape(bsz, L, SSM_G, SSM_P)
    ar, ai = A_re.astype(f32), A_im.astype(f32)
    dt = jnp.exp(log_step.astype(f32))[:, None]
    mag = jnp.exp(ar * dt)
    abar_r, abar_i = mag * jnp.cos(ai * dt), mag * jnp.sin(ai * dt)
    den = ar * ar + ai * ai
    pr, pi_ = abar_r - 1.0, abar_i
    coef_r = (pr * ar + pi_ * ai) / den
    coef_i = (pi_ * ar - pr * ai) / den
    Br, Bi = B_re.astype(f32), B_im.astype(f32)
    bbar_r = coef_r[..., None] * Br - coef_i[..., None] * Bi
    bbar_i = coef_r[..., None] * Bi + coef_i[..., None] * Br
    br = jnp.einsum('blgp,gnp->blgn', uf, bbar_r)
    bi = jnp.einsum('blgp,gnp->blgn', uf, bbar_i)
    a_r = jnp.broadcast_to(abar_r[None, None], (1, L, SSM_G, SSM_N))
    a_i = jnp.broadcast_to(abar_i[None, None], (1, L, SSM_G, SSM_N))
    _, _, xr, xi = lax.associative_scan(_complex_linear_recurrence, (a_r, a_i, br, bi), axis=1)
    y = (jnp.einsum('blgn,gpn->blgp', xr, C_re.astype(f32))
         - jnp.einsum('blgn,gpn->blgp', xi, C_im.astype(f32))
         + D.astype(f32).reshape(SSM_G, SSM_P) * uf)
    y = jax.nn.gelu(y.reshape(bsz, L, W_SSM))
    y = y * jax.nn.sigmoid(y @ w_glu.astype(f32))
    return y.astype(u.dtype)


def fox_attention(q, k, v, f_logit, b_f, q_gain, k_gain):
    f32 = jnp.float32
    bsz, L, _ = q.shape
    q = rms_norm(q.reshape(bsz, L, FOX_H, FOX_DH), q_gain)
    k = rms_norm(k.reshape(bsz, L, FOX_H, FOX_DH), k_gain)
    v = v.reshape(bsz, L, FOX_H, FOX_DH)
    log_f = jax.nn.log_sigmoid(f_logit.astype(f32) + b_f.astype(f32))
    c = jnp.cumsum(log_f, axis=1).transpose(0, 2, 1)
    qh, kh, vh = (t.transpose(0, 2, 1, 3) for t in (q, k, v))
    nb = L // Q_BLOCK
    qb = qh.reshape(bsz, FOX_H, nb, Q_BLOCK, FOX_DH).transpose(2, 0, 1, 3, 4)
    cb = c.reshape(bsz, FOX_H, nb, Q_BLOCK).transpose(2, 0, 1, 3)
    kpos = jnp.arange(L)
    scale = FOX_DH ** -0.5

    def block(args):
        qblk, cblk, i = args
        qpos = i * Q_BLOCK + jnp.arange(Q_BLOCK)
        s = jnp.einsum('bhqd,bhkd->bhqk', qblk, kh, preferred_element_type=f32) * scale
        s = s + cblk[..., :, None] - c[:, :, None, :]
        s = jnp.where(kpos[None, :] <= qpos[:, None], s, -jnp.inf)
        p = jax.nn.softmax(s, axis=-1)
        return jnp.einsum('bhqk,bhkd->bhqd', p.astype(vh.dtype), vh)

    out = lax.map(block, (qb, cb, jnp.arange(nb)))
    return out.transpose(1, 0, 3, 2, 4).reshape(bsz, L, W_FOX)


def memory_cross_attention(hn, memn, wq, wk, wv, q_gain, k_gain, wo):
    f32 = jnp.float32
    bsz, L, _ = hn.shape
    M = memn.shape[1]
    q = rms_norm((hn @ wq).reshape(bsz, L, X_HEADS, X_DH), q_gain)
    k = rms_norm((memn @ wk).reshape(bsz, M, X_HEADS, X_DH), k_gain)
    v = (memn @ wv).reshape(bsz, M, X_HEADS, X_DH)
    s = jnp.einsum('blhd,bmhd->bhlm', q, k, preferred_element_type=f32) * (X_DH ** -0.5)
    p = jax.nn.softmax(s, axis=-1)
    o = jnp.einsum('bhlm,bmhd->blhd', p.astype(v.dtype), v).reshape(bsz, L, X_W)
    return o @ wo


def setup_inputs(seed: int = 0) -> dict:
    key = jax.random.key(seed)
    ks = iter(jax.random.split(key, 64))
    f32 = jnp.float32

    def nrm(shape, scale):
        return jax.random.normal(next(ks), shape, f32) * scale

    def gain(shape):
        return 1.0 + 0.02 * jax.random.normal(next(ks), shape, f32)

    L = DEPTH
    n_idx = jnp.arange(SSM_N, dtype=f32)
    inp = {}
    inp['x'] = jax.random.normal(next(ks), (BATCH, SEQ, D_MODEL), f32)
    inp['mem'] = jax.random.normal(next(ks), (BATCH, MEM_LEN, D_MODEL), f32)
    inp['ffn1_norm'] = gain((L, D_MODEL))
    inp['ffn1_w_gate'] = nrm((L, D_MODEL, D_FF), D_MODEL ** -0.5)
    inp['ffn1_w_up'] = nrm((L, D_MODEL, D_FF), D_MODEL ** -0.5)
    inp['ffn1_w_down'] = nrm((L, D_FF, D_MODEL), D_FF ** -0.5)
    inp['mix_norm'] = gain((L, D_MODEL))
    inp['w_in'] = nrm((L, D_MODEL, IN_W), D_MODEL ** -0.5)
    inp['ssm_A_re'] = -0.5 + 0.01 * jax.random.normal(next(ks), (L, SSM_G, SSM_N), f32)
    inp['ssm_A_im'] = math.pi * n_idx + 0.01 * jax.random.normal(next(ks), (L, SSM_G, SSM_N), f32)
    inp['ssm_log_step'] = jax.random.uniform(next(ks), (L, SSM_G), f32, math.log(1e-3), math.log(1e-1))
    inp['ssm_B_re'] = nrm((L, SSM_G, SSM_N, SSM_P), (2 * SSM_P) ** -0.5)
    inp['ssm_B_im'] = nrm((L, SSM_G, SSM_N, SSM_P), (2 * SSM_P) ** -0.5)
    inp['ssm_C_re'] = nrm((L, SSM_G, SSM_P, SSM_N), (2 * SSM_N) ** -0.5)
    inp['ssm_C_im'] = nrm((L, SSM_G, SSM_P, SSM_N), (2 * SSM_N) ** -0.5)
    inp['ssm_D'] = nrm((L, W_SSM), 1.0)
    inp['ssm_w_glu'] = nrm((L, W_SSM, W_SSM), W_SSM ** -0.5)
    inp['ssm_out_norm'] = gain((L, W_SSM))
    inp['fox_b_f'] = jax.random.uniform(next(ks), (L, FOX_H), f32, 1.0, 6.0)
    inp['fox_q_norm'] = gain((L, FOX_DH))
    inp['fox_k_norm'] = gain((L, FOX_DH))
    inp['fox_out_norm'] = gain((L, W_FOX))
    inp['w_out'] = nrm((L, MIX_W, D_MODEL), MIX_W ** -0.5)
    inp['xattn_norm'] = gain((L, D_MODEL))
    inp['mem_norm'] = gain((L, D_MODEL))
    inp['xattn_wq'] = nrm((L, D_MODEL, X_W), D_MODEL ** -0.5)
    inp['xattn_wk'] = nrm((L, D_MODEL, X_W), D_MODEL ** -0.5)
    inp['xattn_wv'] = nrm((L, D_MODEL, X_W), D_MODEL ** -0.5)
    inp['xattn_q_norm'] = gain((L, X_DH))
    inp['xattn_k_norm'] = gain((L, X_DH))
    inp['xattn_wo'] = nrm((L, X_W, D_MODEL), X_W ** -0.5)
    inp['ffn2_norm'] = gain((L, D_MODEL))
    inp['ffn2_w_gate'] = nrm((L, D_MODEL, D_FF), D_MODEL ** -0.5)
    inp['ffn2_w_up'] = nrm((L, D_MODEL, D_FF), D_MODEL ** -0.5)
    inp['ffn2_w_down'] = nrm((L, D_FF, D_MODEL), D_FF ** -0.5)
    inp['final_norm'] = gain((L, D_MODEL))
    return inp


def reference(x, mem, ffn1_norm, ffn1_w_gate, ffn1_w_up, ffn1_w_down, mix_norm, w_in,
              ssm_A_re, ssm_A_im, ssm_log_step, ssm_B_re, ssm_B_im, ssm_C_re, ssm_C_im,
              ssm_D, ssm_w_glu, ssm_out_norm, fox_b_f, fox_q_norm, fox_k_norm, fox_out_norm,
              w_out, xattn_norm, mem_norm, xattn_wq, xattn_wk, xattn_wv, xattn_q_norm,
              xattn_k_norm, xattn_wo, ffn2_norm, ffn2_w_gate, ffn2_w_up, ffn2_w_down, final_norm):
    h = x
    o_q = W_SSM
    o_k = o_q + W_FOX
    o_v = o_k + W_FOX
    o_f = o_v + W_FOX
    for l in range(DEPTH):
        h = h + 0.5 * swiglu(rms_norm(h, ffn1_norm[l]), ffn1_w_gate[l], ffn1_w_up[l], ffn1_w_down[l])
        z = rms_norm(h, mix_norm[l]) @ w_in[l]
        y_ssm = s5_mixer(z[..., :o_q], ssm_A_re[l], ssm_A_im[l], ssm_log_step[l],
                         ssm_B_re[l], ssm_B_im[l], ssm_C_re[l], ssm_C_im[l],
                         ssm_D[l], ssm_w_glu[l])
        y_fox = fox_attention(z[..., o_q:o_k], z[..., o_k:o_v], z[..., o_v:o_f], z[..., o_f:],
                              fox_b_f[l], fox_q_norm[l], fox_k_norm[l])
        y = jnp.concatenate([rms_norm(y_ssm, ssm_out_norm[l]),
                             rms_norm(y_fox, fox_out_norm[l])], axis=-1)
        h = h + y @ w_out[l]
        h = h + memory_cross_attention(rms_norm(h, xattn_norm[l]), rms_norm(mem, mem_norm[l]),
                                       xattn_wq[l], xattn_wk[l], xattn_wv[l],
                                       xattn_q_norm[l], xattn_k_norm[l], xattn_wo[l])
        h = h + 0.5 * swiglu(rms_norm(h, ffn2_norm[l]), ffn2_w_gate[l], ffn2_w_up[l], ffn2_w_down[l])
        h = rms_norm(h, final_norm[l])
    return h
```

```python
import math
from contextlib import ExitStack, contextmanager

import numpy as np
import concourse.bass as bass
import concourse.mybir as mybir
from concourse.bass_utils import run_bass_kernel_spmd

F32 = mybir.dt.float32
BF16 = mybir.dt.bfloat16
AF = mybir.ActivationFunctionType
ALU = mybir.AluOpType
EPS = 1e-6


class Tok:
    __slots__ = ("sem", "val")

    def __init__(self, sem, val):
        self.sem = sem
        self.val = val


class Buf:
    def __init__(self, fw, name):
        self.fw = fw
        self.name = name
        self.w = None
        self.r = []
        self.sem = None
        self.cnt = 0
        self.pend = []

    def dsem(self):
        if self.sem is None:
            self.sem = self.fw.new_sem()
        return self.sem


class Eng:
    def __init__(self, fw, name):
        self.name = name
        self.sem = fw.new_sem()
        self.cnt = 0
        self.q = []
        self.seen = {}
        self.pending = []


class FW:
    ENGS = ("sync", "gpsimd", "tensor", "scalar", "vector")

    def __init__(self, nc, es):
        self.nc = nc
        self.es = es
        self.sem_pool = []
        self.nsem = 0
        self.eng = {n: Eng(self, n) for n in self.ENGS}
        self.dram_tok = {}
        self.phase_dram = set()
        self.phase_bufs = []

    def new_sem(self):
        if self.sem_pool:
            return self.sem_pool.pop()
        self.nsem += 1
        h = self.es.enter_context(self.nc.semaphore(f"s{self.nsem}"))
        return [h, 0]

    def buf(self, name):
        b = Buf(self, name)
        self.phase_bufs.append(b)
        return b

    def _collect(self, E, reads, writes, nowaw):
        waits = {}

        def need(tok):
            if tok is None:
                return
            k = id(tok.sem)
            cur = waits.get(k)
            if cur is None or cur[1] < tok.val:
                waits[k] = (tok.sem, tok.val)

        for b in reads:
            for (pe, kind) in b.pend:
                assert pe is E or kind == "r", f"read of {b.name}: pending write on {pe.name}"
            need(b.w)
        for b in writes:
            for (pe, kind) in b.pend:
                assert pe is E, f"write of {b.name}: pending access on {pe.name}"
            if not nowaw:
                need(b.w)
            for t in b.r:
                need(t)
        wl = []
        for k, (sem, val) in waits.items():
            if E.seen.get(k, 0) >= val:
                continue
            E.seen[k] = val
            wl.append((sem[0], val))
        return wl

    def op(self, eng, fn, reads=(), writes=(), inc=True, nowaw=False):
        E = self.eng[eng]
        wl = self._collect(E, reads, writes, nowaw)
        for b in reads:
            E.pending.append((b, "r"))
            b.pend.append((E, "r"))
        for b in writes:
            E.pending.append((b, "w"))
            b.pend.append((E, "w"))
        if inc:
            E.sem[1] += 1
            tok = Tok(E.sem, E.sem[1])
            for (b, kind) in E.pending:
                if kind == "w":
                    b.w = tok
                    b.r = []
                else:
                    b.r = [t for t in b.r if t.sem is not tok.sem] + [tok]
                if b.pend:
                    b.pend = [p for p in b.pend if p[0] is not E]
            E.pending = []
            E.q.append((wl, fn, (E.sem[0], 1)))
        else:
            E.q.append((wl, fn, None))

    def dma(self, eng, fn, reads=(), write=None, dram_out=None, nowaw=False):
        E = self.eng[eng]
        writes = (write,) if write is not None else ()
        wl = self._collect(E, reads, writes, nowaw)
        if write is not None:
            sem = write.dsem()
        else:
            if dram_out not in self.dram_tok:
                self.dram_tok[dram_out] = self.new_sem()
            sem = self.dram_tok[dram_out]
            self.phase_dram.add(dram_out)
        sem[1] += 16
        tok = Tok(sem, sem[1])
        if write is not None:
            write.w = tok
            write.r = []
        for b in reads:
            b.r = [t for t in b.r if t.sem is not tok.sem] + [tok]
        E.q.append((wl, fn, (sem[0], 16)))

    def flush(self):
        for E in self.eng.values():
            assert not E.pending, f"engine {E.name} has pending non-inc ops at flush"
        final_waits = []
        for dn in sorted(self.phase_dram):
            sem = self.dram_tok[dn]
            final_waits.append((sem[0], sem[1]))
        for b in self.phase_bufs:
            if b.sem is not None:
                final_waits.append((b.sem[0], b.sem[1]))
        self.phase_dram = set()
        engs = self.eng
        with self.nc.Block() as blk:
            def run(E, e, extra=()):
                for (wl, fn, inc) in E.q:
                    for (sem, val) in wl:
                        e.wait_ge(sem, val)
                    ins = fn(e)
                    if inc is not None:
                        ins.then_inc(inc[0], inc[1])
                for (sem, val) in extra:
                    e.wait_ge(sem, val)
                E.q = []

            @blk.sync
            def _(e):
                run(engs["sync"], e, final_waits)

            @blk.gpsimd
            def _(e):
                run(engs["gpsimd"], e)

            @blk.tensor
            def _(e):
                run(engs["tensor"], e)

            @blk.scalar
            def _(e):
                run(engs["scalar"], e)

            @blk.vector
            def _(e):
                run(engs["vector"], e)
        for b in self.phase_bufs:
            if b.sem is not None:
                self.sem_pool.append(b.sem)
                b.sem = None
        self.phase_bufs = []


class Pool:
    def __init__(self, mk, es, name, shape, dt, n):
        self.items = []
        for i in range(n):
            t = es.enter_context(mk.nc.sbuf_tensor(f"{name}{i}_{mk.uid()}", shape, dt))
            self.items.append((t, mk.fw.buf(f"{name}{i}")))
        self.i = 0

    def get(self):
        it = self.items[self.i % len(self.items)]
        self.i += 1
        return it


class Cfg:
    def __init__(self, D=4096, T=4096, depth=2, mem=256):
        self.D = D
        self.T = T
        self.depth = depth
        self.MEM = mem
        self.WS = D // 2
        self.P = 16
        self.G = self.WS // 16
        self.N = 64
        self.DH = 128
        self.WF = D - self.WS
        self.H = self.WF // 128
        self.INW = self.WS + 3 * self.WF + self.H
        self.DFF = ((8 * D // 3 + 255) // 256) * 256
        self.XH = 4
        self.XDH = D // 4


GAIN_NAMES = ["ffn1_norm", "mix_norm", "ssm_out_norm", "fox_q_norm", "fox_k_norm", "fox_out_norm",
              "xattn_norm", "mem_norm", "xattn_q_norm", "xattn_k_norm", "ffn2_norm", "final_norm"]


class MK:
    def __init__(self, cfg, debug_outs=()):
        self.cfg = cfg
        self.nc = bass.Bass("TRN2", target_bir_lowering=False)
        self._uid = 0
        self.debug_outs = set(debug_outs)
        self.dram = {}

    def uid(self):
        self._uid += 1
        return self._uid

    def din(self, name, shape, dt=F32):
        self.dram[name] = self.nc.dram_tensor(name, list(shape), dt, kind="ExternalInput").ap()
        return self.dram[name]

    def dscr(self, name, shape, dt):
        kind = "ExternalOutput" if name in self.debug_outs else "Internal"
        self.dram[name] = self.nc.dram_tensor(name, list(shape), dt, kind=kind).ap()
        return self.dram[name]

    @contextmanager
    def phase(self):
        with ExitStack() as es:
            yield es
            self.fw.flush()

    def norm(self, src, dst, C, T, group, gcol0, gper, dst_name, srows=0, drows=0):
        nc, fw = self.nc, self.fw
        G = group
        inv = 1.0 / (G * 128)
        TW = min(512, T)
        with self.phase() as es:
            xin = Pool(self, es, "nx", [128, TW], src.dtype, 6)
            sqp = Pool(self, es, "nsq", [128, TW], BF16, 4)
            rp = Pool(self, es, "nr", [128, TW], F32, 2)
            yo = Pool(self, es, "ny", [128, TW], dst.dtype, 4)
            ps = es.enter_context(nc.psum_tensor(f"nps{self.uid()}", [128, 2, TW], F32))
            pb = [fw.buf("npb0"), fw.buf("npb1")]
            it = 0
            for t0 in range(0, T, TW):
                for g0 in range(0, C, G):
                    bi = it % 2
                    it += 1
                    for c in range(g0, g0 + G):
                        xt, xb = xin.get()
                        sq, sb = sqp.get()
                        fw.dma("sync", lambda e, xt=xt, c=c, t0=t0: e.dma_start(
                            out=xt[:], in_=src[srows + c * 128: srows + (c + 1) * 128, t0:t0 + TW]), write=xb)
                        fw.op("scalar", lambda e, xt=xt, sq=sq: e.activation(out=sq[:], in_=xt[:], func=AF.Square),
                              reads=[xb], writes=[sb])
                        fw.op("tensor", lambda e, sq=sq, c=c, bi=bi, g0=g0: e.matmul(
                            ps[:, bi, :], lhsT=self.ones_bf[:, 0:128], rhs=sq[:], start=(c == g0), stop=(c == g0 + G - 1)),
                            reads=[sb, self.const_b], writes=[pb[bi]], inc=True)
                    rt, rb = rp.get()
                    fw.op("vector", lambda e, rt=rt, bi=bi: e.tensor_scalar(
                        out=rt[:], in0=ps[:, bi, :], scalar1=inv, scalar2=EPS, op0=ALU.mult, op1=ALU.add),
                        reads=[pb[bi]], writes=[rb])
                    fw.op("scalar", lambda e, rt=rt: e.activation(out=rt[:], in_=rt[:], func=AF.Sqrt),
                          reads=[rb], writes=[rb])
                    fw.op("vector", lambda e, rt=rt: e.reciprocal(out=rt[:], in_=rt[:]), reads=[rb], writes=[rb])
                    for c in range(g0, g0 + G):
                        xt, xb = xin.get()
                        yt, yb = yo.get()
                        gc = gcol0 + (c % gper)
                        fw.dma("sync", lambda e, xt=xt, c=c, t0=t0: e.dma_start(
                            out=xt[:], in_=src[srows + c * 128: srows + (c + 1) * 128, t0:t0 + TW]), write=xb)
                        fw.op("vector", lambda e, xt=xt, yt=yt, rt=rt, gc=gc: e.scalar_tensor_tensor(
                            out=yt[:], in0=xt[:], scalar=self.gains[:, gc:gc + 1], in1=rt[:], op0=ALU.mult, op1=ALU.mult),
                            reads=[xb, rb, self.const_b], writes=[yb])
                        fw.dma("gpsimd", lambda e, yt=yt, c=c, t0=t0: e.dma_start(
                            out=dst[drows + c * 128: drows + (c + 1) * 128, t0:t0 + TW], in_=yt[:]),
                            reads=[yb], dram_out=dst_name)

    def gemm(self, a, K, Ws, F, T, TG, mk_epi, wcol0=0):
        nc, fw = self.nc, self.fw
        KC = K // 128
        FC = (F + 127) // 128
        TG = min(TG, T)
        TW = min(512, TG)
        NT = TG // TW
        nW = len(Ws)
        a_v = a.rearrange("(kc p) t -> p kc t", p=128)
        w_v = [W.rearrange("(kc p) f -> p kc f", p=128) for W in Ws]
        with self.phase() as es:
            epi = mk_epi(es)
            abuf = es.enter_context(nc.sbuf_tensor(f"abuf{self.uid()}", [128, KC, TG], BF16))
            aparts = [fw.buf(f"a{i}") for i in range((KC + 7) // 8)]
            nslot = 2
            wsl = [[(es.enter_context(nc.sbuf_tensor(f"w{s}_{i}_{self.uid()}", [128, KC, 128], BF16)),
                     fw.buf(f"w{s}_{i}")) for i in range(nW)] for s in range(nslot)]
            ps = es.enter_context(nc.psum_tensor(f"gps{self.uid()}", [128, 8, 512], F32))
            banks = [fw.buf(f"bank{i}") for i in range(8)]
            bi = 0
            wit = 0
            for t0 in range(0, T, TG):
                for i in range(len(aparts)):
                    k0, k1 = i * 8, min(KC, i * 8 + 8)
                    fw.dma("sync", lambda e, k0=k0, k1=k1, t0=t0: e.dma_start(
                        out=abuf[:, k0:k1, :], in_=a_v[:, k0:k1, t0:t0 + TG]), write=aparts[i])
                for fc in range(FC):
                    f0 = fc * 128
                    fw_ = min(128, F - f0)
                    slot = wsl[wit % nslot]
                    wit += 1
                    for wi in range(nW):
                        wt, wb = slot[wi]
                        for pi, k0 in enumerate(range(0, KC, 16)):
                            k1 = min(KC, k0 + 16)
                            fw.dma("gpsimd", lambda e, wt=wt, wi=wi, k0=k0, k1=k1, fw_=fw_, f0=f0: e.dma_start(
                                out=wt[:, k0:k1, 0:fw_], in_=w_v[wi][:, k0:k1, wcol0 + f0:wcol0 + f0 + fw_]),
                                write=wb, nowaw=(pi > 0))
                    for tt in range(NT):
                        bl = []
                        for wi in range(nW):
                            wt, wb = slot[wi]
                            b = bi % 8
                            bi += 1
                            for kc in range(KC):
                                fw.op("tensor", lambda e, wt=wt, kc=kc, b=b, tt=tt, fw_=fw_: e.matmul(
                                    ps[0:fw_, b, 0:TW], lhsT=wt[:, kc, 0:fw_], rhs=abuf[:, kc, tt * TW:(tt + 1) * TW],
                                    start=(kc == 0), stop=(kc == KC - 1)),
                                    reads=[aparts[kc // 8], wb] if kc % 8 == 0 else (), writes=[banks[b]] if kc == 0 else (),
                                    inc=(kc == KC - 1))
                            bl.append((ps[0:fw_, b, 0:TW], banks[b]))
                        epi(fc, fw_, t0 + tt * TW, TW, bl)

    def epi_store(self, dst, dst_name, drow0=0):
        def mk(es):
            fw = self.fw
            op_ = Pool(self, es, "eo", [128, 512], dst.dtype, 4)
            st = {"i": 0}

            def epi(fc, fw_, t0, tw, bl):
                (p_ap, pb), = bl
                ot, ob = op_.get()
                st["i"] += 1
                if st["i"] % 2 == 0:
                    fw.op("scalar", lambda e: e.activation(out=ot[0:fw_, 0:tw], in_=p_ap, func=AF.Copy), reads=[pb], writes=[ob])
                else:
                    fw.op("vector", lambda e: e.tensor_copy(out=ot[0:fw_, 0:tw], in_=p_ap), reads=[pb], writes=[ob])
                r0 = drow0 + fc * 128
                fw.dma("sync", lambda e: e.dma_start(out=dst[r0:r0 + fw_, t0:t0 + tw], in_=ot[0:fw_, 0:tw]),
                       reads=[ob], dram_out=dst_name)
            return epi
        return mk

    def epi_swiglu(self, dst, dst_name):
        def mk(es):
            fw = self.fw
            sp = Pool(self, es, "es", [128, 512], F32, 3)
            op_ = Pool(self, es, "eo", [128, 512], BF16, 4)

            def epi(fc, fw_, t0, tw, bl):
                (g_ap, gb), (u_ap, ub) = bl
                stt, sb = sp.get()
                ot, ob = op_.get()
                fw.op("scalar", lambda e: e.activation(out=stt[0:fw_, 0:tw], in_=g_ap, func=AF.Silu), reads=[gb], writes=[sb])
                fw.op("vector", lambda e: e.tensor_tensor(out=ot[0:fw_, 0:tw], in0=u_ap, in1=stt[0:fw_, 0:tw], op=ALU.mult),
                      reads=[ub, sb], writes=[ob])
                fw.dma("sync", lambda e: e.dma_start(out=dst[fc * 128:fc * 128 + fw_, t0:t0 + tw], in_=ot[0:fw_, 0:tw]),
                       reads=[ob], dram_out=dst_name)
            return epi
        return mk

    def epi_resid(self, hsrc, hdst, hdst_name, alpha):
        def mk(es):
            fw = self.fw
            hp = Pool(self, es, "eh", [128, 512], F32, 4)

            def epi(fc, fw_, t0, tw, bl):
                (p_ap, pb), = bl
                ht, hb = hp.get()
                fw.dma("sync", lambda e: e.dma_start(out=ht[0:fw_, 0:tw], in_=hsrc[fc * 128:fc * 128 + fw_, t0:t0 + tw]), write=hb)
                fw.op("vector", lambda e: e.scalar_tensor_tensor(
                    out=ht[0:fw_, 0:tw], in0=p_ap, scalar=float(alpha), in1=ht[0:fw_, 0:tw], op0=ALU.mult, op1=ALU.add),
                    reads=[pb, hb], writes=[hb])
                fw.dma("sync", lambda e: e.dma_start(out=hdst[fc * 128:fc * 128 + fw_, t0:t0 + tw], in_=ht[0:fw_, 0:tw]),
                       reads=[hb], dram_out=hdst_name)
            return epi
        return mk

    def epi_glu(self, gsrc, dst, dst_name):
        def mk(es):
            fw = self.fw
            hp = Pool(self, es, "eg", [128, 512], F32, 4)
            sp = Pool(self, es, "es", [128, 512], F32, 3)

            def epi(fc, fw_, t0, tw, bl):
                (p_ap, pb), = bl
                ht, hb = hp.get()
                stt, sb = sp.get()
                fw.dma("sync", lambda e: e.dma_start(out=ht[0:fw_, 0:tw], in_=gsrc[fc * 128:fc * 128 + fw_, t0:t0 + tw]), write=hb)
                fw.op("scalar", lambda e: e.activation(out=stt[0:fw_, 0:tw], in_=p_ap, func=AF.Sigmoid), reads=[pb], writes=[sb])
                fw.op("vector", lambda e: e.tensor_tensor(out=ht[0:fw_, 0:tw], in0=ht[0:fw_, 0:tw], in1=stt[0:fw_, 0:tw], op=ALU.mult),
                      reads=[hb, sb], writes=[hb])
                fw.dma("sync", lambda e: e.dma_start(out=dst[fc * 128:fc * 128 + fw_, t0:t0 + tw], in_=ht[0:fw_, 0:tw]),
                       reads=[hb], dram_out=dst_name)
            return epi
        return mk

    def _sincos(self, fw, eng_v, ang, ang_b, P, n, mk_tile, outs):
        sin_t, sin_b, cos_t, cos_b = outs
        TWO_PI = 2.0 * math.pi
        C1 = 6.28125
        C2 = TWO_PI - C1
        kf, kfb = mk_tile(F32)
        ki, kib = mk_tile(mybir.dt.int32)
        ph, phb = mk_tile(F32)
        sy, syb = mk_tile(F32)
        sh, shb = mk_tile(F32)
        fw.op(eng_v, lambda e: e.tensor_scalar(out=kf[0:P, 0:n], in0=ang[0:P, 0:n], scalar1=1.0 / TWO_PI, scalar2=None,
                                               op0=ALU.mult), reads=[ang_b], writes=[kfb])
        fw.op(eng_v, lambda e: e.tensor_copy(out=ki[0:P, 0:n], in_=kf[0:P, 0:n]), reads=[kfb], writes=[kib])
        fw.op(eng_v, lambda e: e.tensor_copy(out=kf[0:P, 0:n], in_=ki[0:P, 0:n]), reads=[kib], writes=[kfb])
        fw.op(eng_v, lambda e: e.scalar_tensor_tensor(out=ph[0:P, 0:n], in0=kf[0:P, 0:n], scalar=-C1, in1=ang[0:P, 0:n],
                                                      op0=ALU.mult, op1=ALU.add), reads=[kfb, ang_b], writes=[phb])
        fw.op(eng_v, lambda e: e.scalar_tensor_tensor(out=ph[0:P, 0:n], in0=kf[0:P, 0:n], scalar=-C2, in1=ph[0:P, 0:n],
                                                      op0=ALU.mult, op1=ALU.add), reads=[kfb, phb], writes=[phb])
        fw.op("scalar", lambda e: e.activation(out=sy[0:P, 0:n], in_=ph[0:P, 0:n], func=AF.Sin, scale=0.5),
              reads=[phb], writes=[syb])
        fw.op("scalar", lambda e: e.activation(out=sh[0:P, 0:n], in_=ph[0:P, 0:n], func=AF.Sin, scale=0.25),
              reads=[phb], writes=[shb])
        fw.op(eng_v, lambda e: e.tensor_tensor(out=sh[0:P, 0:n], in0=sh[0:P, 0:n], in1=sh[0:P, 0:n], op=ALU.mult),
              reads=[shb], writes=[shb])
        fw.op(eng_v, lambda e: e.tensor_scalar(out=sh[0:P, 0:n], in0=sh[0:P, 0:n], scalar1=-2.0, scalar2=1.0,
                                               op0=ALU.mult, op1=ALU.add), reads=[shb], writes=[shb])
        fw.op(eng_v, lambda e: e.scalar_tensor_tensor(out=sin_t[0:P, 0:n], in0=sy[0:P, 0:n], scalar=2.0, in1=sh[0:P, 0:n],
                                                      op0=ALU.mult, op1=ALU.mult), reads=[syb, shb], writes=[sin_b])
        fw.op(eng_v, lambda e: e.tensor_tensor(out=sy[0:P, 0:n], in0=sy[0:P, 0:n], in1=sy[0:P, 0:n], op=ALU.mult),
              reads=[syb], writes=[syb])
        fw.op(eng_v, lambda e: e.tensor_tensor(out=sh[0:P, 0:n], in0=sh[0:P, 0:n], in1=sh[0:P, 0:n], op=ALU.mult),
              reads=[shb], writes=[shb])
        fw.op(eng_v, lambda e: e.tensor_tensor(out=cos_t[0:P, 0:n], in0=sh[0:P, 0:n], in1=sy[0:P, 0:n], op=ALU.subtract),
              reads=[syb, shb], writes=[cos_b])

    def s5(self, z, yssm, l):
        nc, fw, cfg = self.nc, self.fw, self.cfg
        T = cfg.T
        NP = cfg.G // 2
        TC = min(512, T)
        NCH = T // TC
        dr = self.dram
        with self.phase() as es:
            sb = lambda name, shape, dt: es.enter_context(nc.sbuf_tensor(f"{name}_{self.uid()}", shape, dt))
            par = {}
            for n in ("are", "aim", "lst"):
                t = sb("p" + n, [128, NP], F32)
                b = fw.buf("p" + n)
                fw.dma("sync", lambda e, t=t, n=n: e.dma_start(out=t[:], in_=dr["s5_" + n][l]), write=b)
                par[n] = (t, b)
            bre = sb("bre", [32, NP * 128], BF16); breb = fw.buf("bre")
            bim = sb("bim", [32, NP * 128], BF16); bimb = fw.buf("bim")
            cre = sb("cre", [128, NP * 32], BF16); creb = fw.buf("cre")
            cim = sb("cim", [128, NP * 32], BF16); cimb = fw.buf("cim")
            dcol = sb("dcol", [32, NP], F32); dcolb = fw.buf("dcol")
            fw.dma("gpsimd", lambda e: e.dma_start(out=bre[:], in_=dr["s5_bre"][l]), write=breb)
            fw.dma("gpsimd", lambda e: e.dma_start(out=bim[:], in_=dr["s5_bim"][l]), write=bimb)
            fw.dma("gpsimd", lambda e: e.dma_start(out=cre[:], in_=dr["s5_cre"][l]), write=creb)
            fw.dma("gpsimd", lambda e: e.dma_start(out=cim[:], in_=dr["s5_cim"][l]), write=cimb)
            fw.dma("sync", lambda e: e.dma_start(out=dcol[:], in_=dr["s5_d"][l]), write=dcolb)

            def small(dt=F32):
                t = sb("sm", [128, max(NP, 1)], dt)
                return t, fw.buf("sm")

            are, areb = par["are"]; aim, aimb = par["aim"]; lst, lstb = par["lst"]
            dt_, dtb = small(); r_, rb_ = small(); th, thb = small()
            fw.op("scalar", lambda e: e.activation(out=dt_[:], in_=lst[:], func=AF.Exp), reads=[lstb], writes=[dtb])
            fw.op("vector", lambda e: e.tensor_tensor(out=r_[:], in0=are[:], in1=dt_[:], op=ALU.mult), reads=[areb, dtb], writes=[rb_])
            fw.op("scalar", lambda e: e.activation(out=r_[:], in_=r_[:], func=AF.Exp), reads=[rb_], writes=[rb_])
            fw.op("vector", lambda e: e.tensor_tensor(out=th[:], in0=aim[:], in1=dt_[:], op=ALU.mult), reads=[aimb, dtb], writes=[thb])
            sn, snb = small(); cs, csb = small()
            self._sincos(fw, "vector", th, thb, 128, NP, small, (sn, snb, cs, csb))
            thc, thcb = small(); snc, sncb = small(); csc, cscb = small()
            fw.op("vector", lambda e: e.tensor_scalar(out=thc[:], in0=th[:], scalar1=float(TC), scalar2=None, op0=ALU.mult),
                  reads=[thb], writes=[thcb])
            self._sincos(fw, "vector", thc, thcb, 128, NP, small, (snc, sncb, csc, cscb))
            pr_, prb = small(); pi_, pib = small(); den, denb = small(); cr, crb = small(); ci, cib = small()
            t1, t1b = small(); ncr, ncrb = small()
            V = "vector"
            fw.op(V, lambda e: e.tensor_tensor(out=pr_[:], in0=r_[:], in1=cs[:], op=ALU.mult), reads=[rb_, csb], writes=[prb])
            fw.op(V, lambda e: e.tensor_scalar(out=pr_[:], in0=pr_[:], scalar1=-1.0, scalar2=None, op0=ALU.add), reads=[prb], writes=[prb])
            fw.op(V, lambda e: e.tensor_tensor(out=pi_[:], in0=r_[:], in1=sn[:], op=ALU.mult), reads=[rb_, snb], writes=[pib])
            fw.op(V, lambda e: e.tensor_tensor(out=den[:], in0=are[:], in1=are[:], op=ALU.mult), reads=[areb], writes=[denb])
            fw.op(V, lambda e: e.tensor_tensor(out=t1[:], in0=aim[:], in1=aim[:], op=ALU.mult), reads=[aimb], writes=[t1b])
            fw.op(V, lambda e: e.tensor_tensor(out=den[:], in0=den[:], in1=t1[:], op=ALU.add), reads=[denb, t1b], writes=[denb])
            fw.op(V, lambda e: e.reciprocal(out=den[:], in_=den[:]), reads=[denb], writes=[denb])
            fw.op(V, lambda e: e.tensor_tensor(out=cr[:], in0=pr_[:], in1=are[:], op=ALU.mult), reads=[prb, areb], writes=[crb])
            fw.op(V, lambda e: e.tensor_tensor(out=t1[:], in0=pi_[:], in1=aim[:], op=ALU.mult), reads=[pib, aimb], writes=[t1b])
            fw.op(V, lambda e: e.tensor_tensor(out=cr[:], in0=cr[:], in1=t1[:], op=ALU.add), reads=[crb, t1b], writes=[crb])
            fw.op(V, lambda e: e.tensor_tensor(out=cr[:], in0=cr[:], in1=den[:], op=ALU.mult), reads=[crb, denb], writes=[crb])
            fw.op(V, lambda e: e.tensor_tensor(out=ci[:], in0=pi_[:], in1=are[:], op=ALU.mult), reads=[pib, areb], writes=[cib])
            fw.op(V, lambda e: e.tensor_tensor(out=t1[:], in0=pr_[:], in1=aim[:], op=ALU.mult), reads=[prb, aimb], writes=[t1b])
            fw.op(V, lambda e: e.tensor_tensor(out=ci[:], in0=ci[:], in1=t1[:], op=ALU.subtract), reads=[cib, t1b], writes=[cib])
            fw.op(V, lambda e: e.tensor_tensor(out=ci[:], in0=ci[:], in1=den[:], op=ALU.mult), reads=[cib, denb], writes=[cib])
            fw.op(V, lambda e: e.tensor_scalar(out=ncr[:], in0=cr[:], scalar1=-1.0, scalar2=None, op0=ALU.mult), reads=[crb], writes=[ncrb])
            jrow = sb("jrow", [128, TC], F32); jrowb = fw.buf("jrow")
            fw.op("gpsimd", lambda e: e.iota(jrow[:], pattern=[[1, TC]], base=0, channel_multiplier=0,
                                             allow_small_or_imprecise_dtypes=True), writes=[jrowb])
            pbuf = fw.buf("s5params")
            fw.op(V, lambda e: e.tensor_copy(out=t1[:, 0:1], in_=ncr[:, 0:1]),
                  reads=[ncrb, crb, cib, rb_, thb, sncb, cscb, jrowb], writes=[t1b, pbuf])

            tabp = Pool(self, es, "tab", [128, TC], F32, 14)
            wkp = Pool(self, es, "wk", [128, TC], F32, 14)
            xbp = Pool(self, es, "xb", [128, TC], BF16, 4)
            ufp = Pool(self, es, "uf", [32, T], F32, 2)
            ubp = Pool(self, es, "ub", [32, T], BF16, 2)
            yop = Pool(self, es, "yo", [32, TC], F32, 3)
            cap = Pool(self, es, "ca", [128, 4], F32, 4)
            ps = es.enter_context(nc.psum_tensor(f"s5ps{self.uid()}", [128, 6, 512], F32))
            pbk = [fw.buf(f"s5bank{i}") for i in range(6)]
            it = 0

            def tile_getter(pool):
                def g(dt=F32):
                    return pool.get()
                return g

            for pr in range(NP):
                ang, angb = tabp.get()
                fw.op(V, lambda e, ang=ang, pr=pr: e.tensor_scalar(out=ang[:], in0=jrow[:], scalar1=th[:, pr:pr + 1], scalar2=None,
                                                                  op0=ALU.mult), reads=[pbuf], writes=[angb])
                sinT, sinb = tabp.get(); cosT, cosb = tabp.get()

                def mk_tile(dt=F32, _es=es):
                    if dt == F32:
                        return wkp.get()
                    t = sb("ki", [128, TC], dt)
                    return t, fw.buf("ki")
                if pr == 0:
                    kint = sb("kint", [128, TC], mybir.dt.int32)
                    kintb = fw.buf("kint")

                def mk_tile2(dt=F32):
                    if dt == F32:
                        return wkp.get()
                    return kint, kintb
                self._sincos(fw, V, ang, angb, 128, TC, mk_tile2, (sinT, sinb, cosT, cosb))
                ncosT, ncosb = tabp.get(); e1r, e1rb = tabp.get(); e1i, e1ib = tabp.get(); rT, rTb = tabp.get()
                G_ = "gpsimd"
                fw.op(G_, lambda e, ncosT=ncosT, cosT=cosT: e.tensor_scalar(out=ncosT[:], in0=cosT[:], scalar1=-1.0, scalar2=None,
                                                                           op0=ALU.mult), reads=[cosb], writes=[ncosb])
                fw.op(V, lambda e, e1r=e1r, cosT=cosT, pr=pr: e.tensor_scalar(out=e1r[:], in0=cosT[:], scalar1=cr[:, pr:pr + 1],
                                                                            scalar2=None, op0=ALU.mult), reads=[cosb, pbuf], writes=[e1rb])
                fw.op(V, lambda e, e1r=e1r, sinT=sinT, pr=pr: e.scalar_tensor_tensor(
                    out=e1r[:], in0=sinT[:], scalar=ci[:, pr:pr + 1], in1=e1r[:], op0=ALU.mult, op1=ALU.add),
                    reads=[sinb, pbuf, e1rb], writes=[e1rb])
                fw.op(V, lambda e, e1i=e1i, cosT=cosT, pr=pr: e.tensor_scalar(out=e1i[:], in0=cosT[:], scalar1=ci[:, pr:pr + 1],
                                                                            scalar2=None, op0=ALU.mult), reads=[cosb, pbuf], writes=[e1ib])
                fw.op(V, lambda e, e1i=e1i, sinT=sinT, pr=pr: e.scalar_tensor_tensor(
                    out=e1i[:], in0=sinT[:], scalar=ncr[:, pr:pr + 1], in1=e1i[:], op0=ALU.mult, op1=ALU.add),
                    reads=[sinb, pbuf, e1ib], writes=[e1ib])
                fw.op(V, lambda e, rT=rT, pr=pr: e.tensor_scalar(out=rT[:], in0=self.ones_f[:, 0:TC], scalar1=r_[:, pr:pr + 1],
                                                                 scalar2=None, op0=ALU.mult), reads=[pbuf, self.const_b], writes=[rTb])
                uf, ufb = ufp.get(); ub, ubb = ubp.get()
                fw.dma("sync", lambda e, uf=uf, pr=pr: e.dma_start(out=uf[:], in_=z[pr * 32:(pr + 1) * 32, 0:T]), write=ufb)
                fw.op("scalar", lambda e, uf=uf, ub=ub: e.activation(out=ub[:], in_=uf[:], func=AF.Copy), reads=[ufb], writes=[ubb])
                ca, cab = cap.get()
                fw.op(V, lambda e, ca=ca: e.memset(ca[:], 0.0), writes=[cab])
                for c in range(NCH):
                    b0 = (it * 3) % 6
                    it += 1
                    c0 = c * TC
                    fw.op("tensor", lambda e, b0=b0, ub=ub, pr=pr, c0=c0: e.matmul(
                        ps[:, b0, 0:TC], lhsT=bre[:, pr * 128:(pr + 1) * 128], rhs=ub[:, c0:c0 + TC], start=True, stop=True),
                        reads=[breb, ubb], writes=[pbk[b0]])
                    fw.op("tensor", lambda e, b0=b0, ub=ub, pr=pr, c0=c0: e.matmul(
                        ps[:, b0 + 1, 0:TC], lhsT=bim[:, pr * 128:(pr + 1) * 128], rhs=ub[:, c0:c0 + TC], start=True, stop=True),
                        reads=[bimb, ubb], writes=[pbk[b0 + 1]])
                    pr_ps = ps[:, b0, 0:TC]; pi_ps = ps[:, b0 + 1, 0:TC]
                    ta, tab_ = wkp.get(); tb, tbb = wkp.get(); bpr, bprb = wkp.get()
                    tc_, tcb = wkp.get(); td, tdb = wkp.get(); bpi, bpib = wkp.get()
                    fw.op(V, lambda e, ta=ta, e1r=e1r, pr_ps=pr_ps: e.tensor_tensor(out=ta[:], in0=pr_ps, in1=e1r[:], op=ALU.mult),
                          reads=[pbk[b0], e1rb], writes=[tab_])
                    fw.op(V, lambda e, tb=tb, e1i=e1i, pi_ps=pi_ps: e.tensor_tensor(out=tb[:], in0=pi_ps, in1=e1i[:], op=ALU.mult),
                          reads=[pbk[b0 + 1], e1ib], writes=[tbb])
                    fw.op(G_, lambda e, bpr=bpr, ta=ta, tb=tb: e.tensor_tensor(out=bpr[:], in0=ta[:], in1=tb[:], op=ALU.subtract),
                          reads=[tab_, tbb], writes=[bprb])
                    fw.op(V, lambda e, tc_=tc_, e1r=e1r, pi_ps=pi_ps: e.tensor_tensor(out=tc_[:], in0=pi_ps, in1=e1r[:], op=ALU.mult),
                          reads=[pbk[b0 + 1], e1rb], writes=[tcb])
                    fw.op(V, lambda e, td=td, e1i=e1i, pr_ps=pr_ps: e.tensor_tensor(out=td[:], in0=pr_ps, in1=e1i[:], op=ALU.mult),
                          reads=[pbk[b0], e1ib], writes=[tdb])
                    fw.op(G_, lambda e, bpi=bpi, tc_=tc_, td=td: e.tensor_tensor(out=bpi[:], in0=tc_[:], in1=td[:], op=ALU.add),
                          reads=[tcb, tdb], writes=[bpib])
                    wr, wrb = wkp.get(); wi, wib = wkp.get()
                    fw.op(V, lambda e, wr=wr, rT=rT, bpr=bpr, ca=ca: e.tensor_tensor_scan(
                        out=wr[:], data0=rT[:], data1=bpr[:], initial=ca[:, 0:1], op0=ALU.mult, op1=ALU.add),
                        reads=[rTb, bprb, cab], writes=[wrb])
                    fw.op(V, lambda e, wi=wi, rT=rT, bpi=bpi, ca=ca: e.tensor_tensor_scan(
                        out=wi[:], data0=rT[:], data1=bpi[:], initial=ca[:, 1:2], op0=ALU.mult, op1=ALU.add),
                        reads=[rTb, bpib, cab], writes=[wib])
                    if c < NCH - 1:
                        ca2, ca2b = cap.get()
                        L_ = TC - 1
                        fw.op(V, lambda e, ca2=ca2, wi=wi, pr=pr: e.tensor_tensor(out=ca2[:, 2:3], in0=wi[:, L_:L_ + 1], in1=snc[:, pr:pr + 1],
                                                                                op=ALU.mult), reads=[wib, pbuf], writes=[ca2b])
                        fw.op(V, lambda e, ca2=ca2, wr=wr, pr=pr: e.scalar_tensor_tensor(
                            out=ca2[:, 0:1], in0=wr[:, L_:L_ + 1], scalar=csc[:, pr:pr + 1], in1=ca2[:, 2:3], op0=ALU.mult, op1=ALU.subtract),
                            reads=[wrb, pbuf, ca2b], writes=[ca2b])
                        fw.op(V, lambda e, ca2=ca2, wr=wr, pr=pr: e.tensor_tensor(out=ca2[:, 3:4], in0=wr[:, L_:L_ + 1], in1=snc[:, pr:pr + 1],
                                                                                op=ALU.mult), reads=[wrb, pbuf, ca2b], writes=[ca2b])
                        fw.op(V, lambda e, ca2=ca2, wi=wi, pr=pr: e.scalar_tensor_tensor(
                            out=ca2[:, 1:2], in0=wi[:, L_:L_ + 1], scalar=csc[:, pr:pr + 1], in1=ca2[:, 3:4], op0=ALU.mult, op1=ALU.add),
                            reads=[wib, pbuf, ca2b], writes=[ca2b])
                    te, teb = wkp.get(); tf, tfb = wkp.get(); tg, tgb = wkp.get(); th_, thb_ = wkp.get()
                    xr, xrb = xbp.get(); nxi, nxib = xbp.get()
                    fw.op(G_, lambda e, te=te, cosT=cosT, wr=wr: e.tensor_tensor(out=te[:], in0=cosT[:], in1=wr[:], op=ALU.mult),
                          reads=[cosb, wrb], writes=[teb])
                    fw.op(G_, lambda e, tf=tf, sinT=sinT, wi=wi: e.tensor_tensor(out=tf[:], in0=sinT[:], in1=wi[:], op=ALU.mult),
                          reads=[sinb, wib], writes=[tfb])
                    fw.op(G_, lambda e, xr=xr, te=te, tf=tf: e.tensor_tensor(out=xr[:], in0=te[:], in1=tf[:], op=ALU.subtract),
                          reads=[teb, tfb], writes=[xrb])
                    fw.op(G_, lambda e, tg=tg, ncosT=ncosT, wi=wi: e.tensor_tensor(out=tg[:], in0=ncosT[:], in1=wi[:], op=ALU.mult),
                          reads=[ncosb, wib], writes=[tgb])
                    fw.op(G_, lambda e, th_=th_, sinT=sinT, wr=wr: e.tensor_tensor(out=th_[:], in0=sinT[:], in1=wr[:], op=ALU.mult),
                          reads=[sinb, wrb], writes=[thb_])
                    fw.op(G_, lambda e, nxi=nxi, tg=tg, th_=th_: e.tensor_tensor(out=nxi[:], in0=tg[:], in1=th_[:], op=ALU.subtract),
                          reads=[tgb, thb_], writes=[nxib])
                    if c < NCH - 1:
                        ca, cab = ca2, ca2b
                    fw.op("tensor", lambda e, b0=b0, xr=xr, pr=pr: e.matmul(
                        ps[0:32, b0 + 2, 0:TC], lhsT=cre[:, pr * 32:(pr + 1) * 32], rhs=xr[:], start=True, stop=False),
                        reads=[creb, xrb], writes=[pbk[b0 + 2]], inc=False)
                    fw.op("tensor", lambda e, b0=b0, nxi=nxi, pr=pr: e.matmul(
                        ps[0:32, b0 + 2, 0:TC], lhsT=cim[:, pr * 32:(pr + 1) * 32], rhs=nxi[:], start=False, stop=True),
                        reads=[cimb, nxib], writes=[pbk[b0 + 2]])
                    yt, ytb = yop.get()
                    fw.op(V, lambda e, yt=yt, uf=uf, pr=pr, c0=c0, b0=b0: e.scalar_tensor_tensor(
                        out=yt[:], in0=uf[:, c0:c0 + TC], scalar=dcol[:, pr:pr + 1], in1=ps[0:32, b0 + 2, 0:TC], op0=ALU.mult, op1=ALU.add),
                        reads=[ufb, dcolb, pbk[b0 + 2]], writes=[ytb])
                    fw.dma("sync", lambda e, yt=yt, pr=pr, c0=c0: e.dma_start(out=yssm[pr * 32:(pr + 1) * 32, c0:c0 + TC], in_=yt[:]),
                           reads=[ytb], dram_out="yssm")

    def fox_gates(self, z, cdram, l):
        nc, fw, cfg = self.nc, self.fw, self.cfg
        T, H = cfg.T, cfg.H
        frow0 = cfg.WS + 3 * cfg.WF
        with self.phase() as es:
            sb = lambda name, shape, dt: es.enter_context(nc.sbuf_tensor(f"{name}_{self.uid()}", shape, dt))
            ft = sb("ft", [H, T], F32); ftb = fw.buf("ft")
            on = sb("on", [H, T], F32); onb = fw.buf("on")
            ct = sb("ct", [H, T], F32); ctb = fw.buf("ct")
            bf = sb("bf", [H, 1], F32); bfb = fw.buf("bf")
            fw.dma("sync", lambda e: e.dma_start(out=ft[:], in_=z[frow0:frow0 + H, 0:T]), write=ftb)
            fw.dma("sync", lambda e: e.dma_start(out=bf[:], in_=self.dram["fox_b_f"][l:l + 1, :].rearrange("o h -> h o"),
                                                 allow_slow_non_contiguous=True), write=bfb)
            fw.op("vector", lambda e: e.tensor_scalar(out=bf[:], in0=bf[:], scalar1=-1.0, scalar2=None, op0=ALU.mult),
                  reads=[bfb], writes=[bfb])
            fw.op("vector", lambda e: e.memset(on[:], 1.0), writes=[onb])
            fw.op("scalar", lambda e: e.activation(out=ft[:], in_=ft[:], func=AF.Exp, bias=bf[:, 0:1], scale=-1.0),
                  reads=[ftb, bfb], writes=[ftb])
            fw.op("scalar", lambda e: e.activation(out=ft[:], in_=ft[:], func=AF.Ln, bias=1.0, scale=1.0),
                  reads=[ftb], writes=[ftb])
            fw.op("vector", lambda e: e.tensor_tensor_scan(out=ct[:], data0=on[:], data1=ft[:], initial=0.0,
                                                           op0=ALU.mult, op1=ALU.add), reads=[onb, ftb], writes=[ctb])
            fw.op("vector", lambda e: e.tensor_scalar(out=ct[:], in0=ct[:], scalar1=-1.0, scalar2=None, op0=ALU.mult),
                  reads=[ctb], writes=[ctb])
            fw.dma("sync", lambda e: e.dma_start(out=cdram[0:H, 0:T], in_=ct[:]), reads=[ctb], dram_out="cdram")

    def fox(self, z, qn, kn, cdram, ofox, l):
        nc, fw, cfg = self.nc, self.fw, self.cfg
        T, H = cfg.T, cfg.H
        NB = T // 128
        vrow0 = cfg.WS + 2 * cfg.WF
        scale = 128 ** -0.5
        with self.phase() as es:
            sb = lambda name, shape, dt: es.enter_context(nc.sbuf_tensor(f"{name}_{self.uid()}", shape, dt))
            cT = sb("cT", [128, H, NB], F32); cTb = fw.buf("cT")
            rB = sb("rB", [128, H * NB], F32); rBb = fw.buf("rB")
            for h in range(H):
                fw.dma("sync", lambda e, h=h: e.dma_start(
                    out=cT[:, h, :], in_=cdram[h:h + 1, :].rearrange("o (b p) -> p (o b)", p=128),
                    allow_slow_non_contiguous=True), write=cTb, nowaw=(h > 0))
            cmid = cdram.rearrange("h (b p) -> p (h b)", p=128)[63:64, :]
            fw.dma("sync", lambda e: e.dma_start(out=rB[:], in_=cmid.partition_broadcast(128),
                                                 allow_slow_non_contiguous=True), write=rBb)
            import os
            dbg = int(os.environ.get("FOX_DBG", "9"))
            if dbg <= 1:
                return
            qp = Pool(self, es, "fq", [128, T], BF16, 2)
            kp = Pool(self, es, "fk", [128, T], BF16, 2)
            vfp = Pool(self, es, "fvf", [128, T], F32, 1)
            vbp = Pool(self, es, "fvb", [128, T], BF16, 1)
            Vp = Pool(self, es, "fV", [128, NB, 128], BF16, 2)
            biasp = Pool(self, es, "fbias", [128, NB], F32, 3)
            ptp = Pool(self, es, "fpt", [128, 128], BF16, 10)
            rdp = Pool(self, es, "frd", [128, 128], F32, 3)
            outp = Pool(self, es, "fout", [128, 512], F32, 3)
            psS = es.enter_context(nc.psum_tensor(f"fpsS{self.uid()}", [128, 2, 4, 128], F32))
            psO = es.enter_context(nc.psum_tensor(f"fpsO{self.uid()}", [128, 4, 128], F32))
            psD = es.enter_context(nc.psum_tensor(f"fpsD{self.uid()}", [128, 4, 128], F32))
            psT = [es.enter_context(nc.psum_tensor(f"fpsT{a}_{self.uid()}", [128, 4, 128], BF16)) for a in range(2)]
            Sb = [[fw.buf(f"S{a}{b}") for b in range(4)] for a in range(2)]
            Ob = [fw.buf(f"O{b}") for b in range(4)]
            Db = [fw.buf(f"D{b}") for b in range(4)]
            Tb = [fw.buf(f"T{b}") for b in range(2)]
            sbank = 0
            oslot = 0
            for h in range(H):
                qt, qb = qp.get(); kt, kb = kp.get(); vf, vfb = vfp.get(); vb, vbb = vbp.get(); Vt, Vb = Vp.get()
                fw.dma("sync", lambda e, qt=qt, h=h: e.dma_start(out=qt[:], in_=qn[h * 128:(h + 1) * 128, 0:T]), write=qb)
                fw.dma("sync", lambda e, kt=kt, h=h: e.dma_start(out=kt[:], in_=kn[h * 128:(h + 1) * 128, 0:T]), write=kb)
                fw.dma("sync", lambda e, vf=vf, h=h: e.dma_start(out=vf[:], in_=z[vrow0 + h * 128: vrow0 + (h + 1) * 128, 0:T]), write=vfb)
                fw.op("scalar", lambda e, vf=vf, vb=vb: e.activation(out=vb[:], in_=vf[:], func=AF.Copy), reads=[vfb], writes=[vbb])
                for b4 in range(0, NB, 4):
                    ts_ = (b4 // 4) % 2
                    nb = min(4, NB - b4)
                    for j in range(nb):
                        fw.op("tensor", lambda e, vb=vb, ts_=ts_, j=j, b4=b4: e.transpose(
                            out=psT[ts_][:, j, :], in_=vb[:, (b4 + j) * 128:(b4 + j + 1) * 128], identity=self.ident[:]),
                            reads=[vbb, self.const_b], writes=[Tb[ts_]], inc=(j == nb - 1))
                    if (b4 // 4) % 2 == 0:
                        fw.op("vector", lambda e, Vt=Vt, ts_=ts_, b4=b4, nb=nb: e.tensor_copy(out=Vt[:, b4:b4 + nb, :], in_=psT[ts_][:, 0:nb, :]),
                              reads=[Tb[ts_]], writes=[Vb])
                    else:
                        fw.op("scalar", lambda e, Vt=Vt, ts_=ts_, b4=b4, nb=nb: e.activation(out=Vt[:, b4:b4 + nb, :], in_=psT[ts_][:, 0:nb, :], func=AF.Copy),
                              reads=[Tb[ts_]], writes=[Vb])
                ot, otb = None, None
                if dbg <= 2:
                    continue
                for i in range(NB if dbg > 3 else 1):
                    bt, btb = biasp.get()
                    fw.op("vector", lambda e, bt=bt, h=h, i=i: e.tensor_scalar(
                        out=bt[:, 0:i + 1], in0=cT[:, h, 0:i + 1], scalar1=-1.0, scalar2=rB[:, h * NB + i:h * NB + i + 1],
                        op0=ALU.mult, op1=ALU.add), reads=[cTb, rBb], writes=[btb])
                    os_ = oslot % 4
                    oslot += 1
                    batches = [list(range(j0, min(i + 1, j0 + 4))) for j0 in range(0, i + 1, 4)]

                    def emit_S(batch, sbk):
                        for jj, j in enumerate(batch):
                            fw.op("tensor", lambda e, sbk=sbk, jj=jj, j=j, i=i, kt=kt, qt=qt: e.matmul(
                                psS[:, sbk, jj, :], lhsT=kt[:, j * 128:(j + 1) * 128], rhs=qt[:, i * 128:(i + 1) * 128],
                                start=True, stop=True), reads=[kb, qb], writes=[Sb[sbk][jj]])
                    emit_S(batches[0], sbank % 2)
                    for bn, batch in enumerate(batches):
                        sbk = sbank % 2
                        sbank += 1
                        if bn + 1 < len(batches):
                            emit_S(batches[bn + 1], sbank % 2)
                        pts = []
                        for jj, j in enumerate(batch):
                            pt, ptb = ptp.get()
                            fw.op("scalar", lambda e, pt=pt, sbk=sbk, jj=jj, j=j, bt=bt: e.activation(
                                out=pt[:], in_=psS[:, sbk, jj, :], func=AF.Exp, bias=bt[:, j:j + 1], scale=scale),
                                reads=[Sb[sbk][jj], btb], writes=[ptb])
                            if j == i:
                                fw.op("gpsimd", lambda e, pt=pt: e.tensor_tensor(out=pt[:], in0=pt[:], in1=self.cmask[:], op=ALU.mult),
                                      reads=[ptb, self.const_b], writes=[ptb])
                            pts.append((pt, ptb, j))
                        for (pt, ptb, j) in pts:
                            fw.op("tensor", lambda e, pt=pt, j=j, os_=os_, Vt=Vt, i=i: e.matmul(
                                psO[:, os_, :], lhsT=Vt[:, j, :], rhs=pt[:], start=(j == 0), stop=(j == i)),
                                reads=[Vb, ptb], writes=[Ob[os_]], inc=False)
                            fw.op("tensor", lambda e, pt=pt, j=j, os_=os_, i=i: e.matmul(
                                psD[:, os_, :], lhsT=self.ones_bf[:, 0:128], rhs=pt[:], start=(j == 0), stop=(j == i)),
                                reads=[ptb, self.const_b], writes=[Db[os_]])
                    rd, rdb = rdp.get()
                    fw.op("vector", lambda e, rd=rd, os_=os_: e.reciprocal(out=rd[:], in_=psD[:, os_, :]), reads=[Db[os_]], writes=[rdb])
                    if i % 4 == 0:
                        ot, otb = outp.get()
                    fw.op("vector", lambda e, ot=ot, rd=rd, os_=os_, i=i: e.tensor_tensor(
                        out=ot[:, (i % 4) * 128:(i % 4 + 1) * 128], in0=psO[:, os_, :], in1=rd[:], op=ALU.mult),
                        reads=[Ob[os_], rdb], writes=[otb])
                    if i % 4 == 3 or i == NB - 1:
                        i0 = (i // 4) * 4
                        n_ = i - i0 + 1
                        fw.dma("sync", lambda e, ot=ot, h=h, i0=i0, n_=n_: e.dma_start(
                            out=ofox[h * 128:(h + 1) * 128, i0 * 128:(i0 + n_) * 128], in_=ot[:, 0:n_ * 128]),
                            reads=[otb], dram_out="ofox")

    def xattn(self, qxn, kxn, vx, ox):
        nc, fw, cfg = self.nc, self.fw, self.cfg
        T, D, M = cfg.T, cfg.D, cfg.MEM
        DC = D // 128
        XC = cfg.XDH // 128
        MT = M // 128
        TW = min(512, T)
        scale = cfg.XDH ** -0.5
        with self.phase() as es:
            sb = lambda name, shape, dt: es.enter_context(nc.sbuf_tensor(f"{name}_{self.uid()}", shape, dt))
            kx = sb("xk", [128, DC, M], BF16); kxb = fw.buf("xk")
            fw.dma("sync", lambda e: e.dma_start(out=kx[:], in_=kxn.rearrange("(c p) m -> p c m", p=128)), write=kxb)
            Vx = sb("xV", [128, MT, D], BF16); Vxb = fw.buf("xV")
            vfp = Pool(self, es, "xvf", [128, M], F32, 3)
            vbp = Pool(self, es, "xvb", [128, M], BF16, 3)
            ps = es.enter_context(nc.psum_tensor(f"xps{self.uid()}", [128, 6, 512], F32))
            pbk = [fw.buf(f"xbank{i}") for i in range(6)]
            psT = [es.enter_context(nc.psum_tensor(f"xpsT{a}_{self.uid()}", [128, 4, 128], BF16)) for a in range(2)]
            Tb = [fw.buf("xT0"), fw.buf("xT1")]
            for c in range(DC):
                vf, vfb = vfp.get(); vb, vbb = vbp.get()
                ts_ = c % 2
                fw.dma("sync", lambda e, vf=vf, c=c: e.dma_start(out=vf[:], in_=vx[c * 128:(c + 1) * 128, 0:M]), write=vfb)
                fw.op("scalar", lambda e, vf=vf, vb=vb: e.activation(out=vb[:], in_=vf[:], func=AF.Copy), reads=[vfb], writes=[vbb])
                for mt in range(MT):
                    fw.op("tensor", lambda e, vb=vb, ts_=ts_, mt=mt: e.transpose(
                        out=psT[ts_][:, mt, :], in_=vb[:, mt * 128:(mt + 1) * 128], identity=self.ident[:]),
                        reads=[vbb, self.const_b], writes=[Tb[ts_]], inc=(mt == MT - 1))
                fw.op("vector", lambda e, ts_=ts_, c=c: e.tensor_copy(out=Vx[:, :, c * 128:(c + 1) * 128], in_=psT[ts_][:, 0:MT, :]),
                      reads=[Tb[ts_]], writes=[Vxb], nowaw=True)
            qp = Pool(self, es, "xq", [128, XC, TW], BF16, 2)
            ptp = Pool(self, es, "xpt", [128, TW], BF16, 2 * MT + 2)
            rdp = Pool(self, es, "xrd", [128, TW], F32, 2)
            op_ = Pool(self, es, "xo", [128, TW], BF16, 4)
            bi = 0
            for t0 in range(0, T, TW):
                for h in range(cfg.XH):
                    qt, qb = qp.get()
                    r0 = h * cfg.XDH
                    fw.dma("sync", lambda e, qt=qt, r0=r0, t0=t0: e.dma_start(
                        out=qt[:], in_=qxn[r0:r0 + cfg.XDH, t0:t0 + TW].rearrange("(c p) t -> p c t", p=128)), write=qb)
                    pts = []
                    for mt in range(MT):
                        b = bi % 6; bi += 1
                        for dc in range(XC):
                            fw.op("tensor", lambda e, b=b, dc=dc, mt=mt, h=h, qt=qt: e.matmul(
                                ps[:, b, 0:TW], lhsT=kx[:, h * XC + dc, mt * 128:(mt + 1) * 128], rhs=qt[:, dc, :],
                                start=(dc == 0), stop=(dc == XC - 1)),
                                reads=[kxb, qb] if dc == 0 else (), writes=[pbk[b]] if dc == 0 else (), inc=(dc == XC - 1))
                        pt, ptb = ptp.get()
                        fw.op("scalar", lambda e, pt=pt, b=b: e.activation(out=pt[:], in_=ps[:, b, 0:TW], func=AF.Exp, scale=scale),
                              reads=[pbk[b]], writes=[ptb])
                        pts.append((pt, ptb))
                    b = bi % 6; bi += 1
                    for mt, (pt, ptb) in enumerate(pts):
                        fw.op("tensor", lambda e, b=b, pt=pt, mt=mt: e.matmul(
                            ps[:, b, 0:TW], lhsT=self.ones_bf[:, 0:128], rhs=pt[:], start=(mt == 0), stop=(mt == MT - 1)),
                            reads=[ptb, self.const_b], writes=[pbk[b]], inc=(mt == MT - 1))
                    rd, rdb = rdp.get()
                    fw.op("vector", lambda e, rd=rd, b=b: e.reciprocal(out=rd[:], in_=ps[:, b, 0:TW]), reads=[pbk[b]], writes=[rdb])
                    for dvc in range(XC):
                        b = bi % 6; bi += 1
                        fc = h * XC + dvc
                        for mt, (pt, ptb) in enumerate(pts):
                            fw.op("tensor", lambda e, b=b, pt=pt, mt=mt, fc=fc: e.matmul(
                                ps[:, b, 0:TW], lhsT=Vx[:, mt, fc * 128:(fc + 1) * 128], rhs=pt[:], start=(mt == 0), stop=(mt == MT - 1)),
                                reads=[ptb, Vxb], writes=[pbk[b]], inc=(mt == MT - 1))
                        ot, otb = op_.get()
                        fw.op("vector", lambda e, ot=ot, b=b, rd=rd: e.tensor_tensor(out=ot[:], in0=ps[:, b, 0:TW], in1=rd[:], op=ALU.mult),
                              reads=[pbk[b], rdb], writes=[otb])
                        fw.dma("sync", lambda e, ot=ot, fc=fc, t0=t0: e.dma_start(out=ox[fc * 128:(fc + 1) * 128, t0:t0 + TW], in_=ot[:]),
                               reads=[otb], dram_out="ox")

    def gelu(self, src, dstf, dstb, C, T):
        fw = self.fw
        TW = min(512, T)
        with self.phase() as es:
            xp = Pool(self, es, "gx", [128, TW], F32, 4)
            bp = Pool(self, es, "gb", [128, TW], BF16, 4)
            for c in range(C):
                for t0 in range(0, T, TW):
                    xt, xb = xp.get(); bt, bb = bp.get()
                    fw.dma("sync", lambda e, xt=xt, c=c, t0=t0: e.dma_start(out=xt[:], in_=src[c * 128:(c + 1) * 128, t0:t0 + TW]), write=xb)
                    fw.op("scalar", lambda e, xt=xt: e.activation(out=xt[:], in_=xt[:], func=AF.Gelu_apprx_tanh), reads=[xb], writes=[xb])
                    fw.op("vector", lambda e, xt=xt, bt=bt: e.tensor_copy(out=bt[:], in_=xt[:]), reads=[xb], writes=[bb])
                    fw.dma("sync", lambda e, xt=xt, c=c, t0=t0: e.dma_start(out=dstf[c * 128:(c + 1) * 128, t0:t0 + TW], in_=xt[:]),
                           reads=[xb], dram_out="gg")
                    fw.dma("gpsimd", lambda e, bt=bt, c=c, t0=t0: e.dma_start(out=dstb[c * 128:(c + 1) * 128, t0:t0 + TW], in_=bt[:]),
                           reads=[bb], dram_out="ggb")


    def setup_consts(self, es, gain_inputs):
        nc, fw = self.nc, self.fw
        cfg = self.cfg
        sb = lambda name, shape, dt: es.enter_context(nc.sbuf_tensor(name, shape, dt))
        self.const_b = fw.buf("consts")
        self.ones_bf = sb("ones_bf", [128, 512], BF16)
        self.ones_f = sb("ones_f", [128, 512], F32)
        self.ident = sb("ident", [128, 128], BF16)
        self.cmask = sb("cmask", [128, 128], BF16)
        self.gcol = {}
        ncol = 0
        for l in range(cfg.depth):
            for n in GAIN_NAMES:
                sz = gain_inputs[n].shape[1]
                self.gcol[(n, l)] = ncol
                ncol += sz // 128
        self.gains = sb("gains", [128, ncol], F32)
        with self.phase() as pes:
            tmp = pes.enter_context(nc.sbuf_tensor("ctmp", [128, 128], F32))
            tb = fw.buf("ctmp")
            cb = self.const_b
            fw.op("vector", lambda e: e.memset(self.ones_bf[:], 1.0), writes=[cb])
            fw.op("vector", lambda e: e.memset(self.ones_f[:], 1.0), writes=[cb])
            fw.op("gpsimd", lambda e: e.iota(tmp[:], pattern=[[1, 128]], base=0, channel_multiplier=-1,
                                             allow_small_or_imprecise_dtypes=True), writes=[tb])
            fw.op("vector", lambda e: e.tensor_single_scalar(out=self.ident[:], in_=tmp[:], scalar=0.0, op=ALU.is_equal),
                  reads=[tb], writes=[cb])
            fw.op("vector", lambda e: e.tensor_single_scalar(out=self.cmask[:], in_=tmp[:], scalar=0.0, op=ALU.is_ge),
                  reads=[tb], writes=[cb])
            gb = fw.buf("gainsb")
            first = True
            for l in range(cfg.depth):
                for n in GAIN_NAMES:
                    sz = gain_inputs[n].shape[1]
                    c0 = self.gcol[(n, l)]
                    src = self.dram[n][l:l + 1, :].rearrange("o (c p) -> p (o c)", p=128)
                    fw.dma("sync", lambda e, c0=c0, sz=sz, src=src: e.dma_start(
                        out=self.gains[:, c0:c0 + sz // 128], in_=src, allow_slow_non_contiguous=True),
                        write=gb, nowaw=not first)
                    first = False
            fw.op("vector", lambda e: e.tensor_copy(out=tmp[:, 0:1], in_=self.gains[:, 0:1]), reads=[gb], writes=[tb, cb])

    def build(self, stop_after=None, skip_to=None):
        cfg = self.cfg
        nc = self.nc
        D, T, L = cfg.D, cfg.T, cfg.depth
        DC = D // 128
        din = self.din
        xT = din("xT", [D, T])
        memT = din("memT", [D, cfg.MEM])
        gain_inputs = {}
        gsz = {"ffn1_norm": D, "mix_norm": D, "ssm_out_norm": cfg.WS, "fox_q_norm": 128, "fox_k_norm": 128,
               "fox_out_norm": cfg.WF, "xattn_norm": D, "mem_norm": D, "xattn_q_norm": cfg.XDH,
               "xattn_k_norm": cfg.XDH, "ffn2_norm": D, "final_norm": D}
        for n in GAIN_NAMES:
            gain_inputs[n] = din(n, [L, gsz[n]])
        wshape = {"ffn1_w_gate": [D, cfg.DFF], "ffn1_w_up": [D, cfg.DFF], "ffn1_w_down": [cfg.DFF, D],
                  "w_in": [D, cfg.INW], "ssm_w_glu": [cfg.WS, cfg.WS], "w_out": [D, D],
                  "xattn_wq": [D, D], "xattn_wk": [D, D], "xattn_wv": [D, D], "xattn_wo": [D, D],
                  "ffn2_w_gate": [D, cfg.DFF], "ffn2_w_up": [D, cfg.DFF], "ffn2_w_down": [cfg.DFF, D]}
        wcache = {}

        def Wg(n, l):
            if (n, l) not in wcache:
                wcache[(n, l)] = din(f"{n}_{l}", wshape[n])
            return wcache[(n, l)]
        W = None
        import os as _os
        if _os.environ.get("DECL_ALL_W"):
            for _n in wshape:
                for _l in range(L):
                    Wg(_n, _l)
        self.W = W
        NP = cfg.G // 2
        din("s5_are", [L, 128, NP]); din("s5_aim", [L, 128, NP]); din("s5_lst", [L, 128, NP])
        din("s5_bre", [L, 32, NP * 128]); din("s5_bim", [L, 32, NP * 128])
        din("s5_cre", [L, 128, NP * 32]); din("s5_cim", [L, 128, NP * 32])
        din("s5_d", [L, 32, NP])
        din("fox_b_f", [L, cfg.H])
        outT = nc.dram_tensor("outT", [D, T], F32, kind="ExternalOutput").ap()
        hT = self.dscr("hT", [D, T], F32)
        hn = self.dscr("hn", [D, T], BF16)
        act = self.dscr("act", [cfg.DFF, T], BF16)
        z = self.dscr("z", [cfg.INW, T], F32)
        yssm = self.dscr("yssm", [cfg.WS, T], F32)
        qn = self.dscr("qn", [cfg.WF, T], BF16)
        kn = self.dscr("kn", [cfg.WF, T], BF16)
        cdram = self.dscr("cdram", [cfg.H, T], F32)
        ofox = self.dscr("ofox", [cfg.WF, T], F32)
        gg = self.dscr("gg", [cfg.WS, T], F32)
        ggb = self.dscr("ggb", [cfg.WS, T], BF16)
        yglu = self.dscr("yglu", [cfg.WS, T], F32)
        ycat = self.dscr("ycat", [D, T], BF16)
        qx = self.dscr("qx", [D, T], F32)
        qxn = self.dscr("qxn", [D, T], BF16)
        memn = self.dscr("memn", [D, cfg.MEM], BF16)
        kx = self.dscr("kx", [D, cfg.MEM], F32)
        kxn = self.dscr("kxn", [D, cfg.MEM], BF16)
        vx = self.dscr("vx", [D, cfg.MEM], F32)
        ox = self.dscr("ox", [D, T], BF16)
        order = ["ffn1", "win", "s5", "fox", "mixout", "xattn", "ffn2", "final"]

        def reached(stage, l):
            return stop_after is not None and stop_after == f"{stage}{l}"

        with ExitStack() as es:
            self.fw = FW(nc, es)
            self.setup_consts(es, gain_inputs)
            hsrc = xT
            gc = self.gcol
            stopped = False
            import os as _os2
            for l in range(int(_os2.environ.get("LSTART", "0")), L):
                last = (l == L - 1)
                XC = cfg.XDH // 128
                if skip_to == "fin":
                    self.norm(hsrc, hT, DC, T, DC, gc[("final_norm", 0)], DC, "hT")
                    self.norm(hT, hT, DC, T, DC, gc[("final_norm", 0)], DC, "hT")
                    self.norm(hT, hn, DC, T, DC, gc[("ffn1_norm", L - 1)], DC, "hn")
                    stopped = True
                    break
                if skip_to == "xattn":
                    self.norm(hsrc, hn, DC, T, DC, gc[("xattn_norm", l)], DC, "hn")
                    self.gemm(hn, D, [Wg("xattn_wq", l)], D, T, 1024, self.epi_store(qx, "qx"))
                    if reached("xq", l):
                        stopped = True
                        break
                    self.norm(qx, qxn, DC, T, XC, gc[("xattn_q_norm", l)], XC, "qxn")
                    self.norm(memT, memn, DC, cfg.MEM, DC, gc[("mem_norm", l)], DC, "memn")
                    self.gemm(memn, D, [Wg("xattn_wk", l)], D, cfg.MEM, 1024, self.epi_store(kx, "kx"))
                    self.norm(kx, kxn, DC, cfg.MEM, XC, gc[("xattn_k_norm", l)], XC, "kxn")
                    self.gemm(memn, D, [Wg("xattn_wv", l)], D, cfg.MEM, 1024, self.epi_store(vx, "vx"))
                    if reached("xkv", l):
                        stopped = True
                        break
                    self.xattn(qxn, kxn, vx, ox)
                    if reached("xcore", l):
                        stopped = True
                        break
                    self.gemm(ox, D, [Wg("xattn_wo", l)], D, T, 1024, self.epi_resid(hsrc, hT, "hT", 1.0))
                    stopped = True
                    break
                self.norm(hsrc, hn, DC, T, DC, gc[("ffn1_norm", l)], DC, "hn")
                self.gemm(hn, D, [Wg("ffn1_w_gate", l), Wg("ffn1_w_up", l)], cfg.DFF, T, 1024, self.epi_swiglu(act, "act"))
                self.gemm(act, cfg.DFF, [Wg("ffn1_w_down", l)], D, T, 512, self.epi_resid(hsrc, hT, "hT", 0.5))
                hsrc = hT
                if reached("ffn1", l):
                    stopped = True
                    break
                self.norm(hT, hn, DC, T, DC, gc[("mix_norm", l)], DC, "hn")
                self.gemm(hn, D, [Wg("w_in", l)], cfg.INW, T, 1024, self.epi_store(z, "z"))
                if reached("win", l):
                    stopped = True
                    break
                self.s5(z, yssm, l)
                if reached("s5", l):
                    stopped = True
                    break
                HC = cfg.WF // 128
                self.norm(z, qn, HC, T, 1, gc[("fox_q_norm", l)], 1, "qn", srows=cfg.WS)
                self.norm(z, kn, HC, T, 1, gc[("fox_k_norm", l)], 1, "kn", srows=cfg.WS + cfg.WF)
                if reached("qk", l):
                    stopped = True
                    break
                self.fox_gates(z, cdram, l)
                if reached("gates", l):
                    stopped = True
                    break
                self.fox(z, qn, kn, cdram, ofox, l)
                if reached("fox", l):
                    stopped = True
                    break
                SC = cfg.WS // 128
                self.gelu(yssm, gg, ggb, SC, T)
                self.gemm(ggb, cfg.WS, [Wg("ssm_w_glu", l)], cfg.WS, T, 1024, self.epi_glu(gg, yglu, "yglu"))
                self.norm(yglu, ycat, SC, T, SC, gc[("ssm_out_norm", l)], SC, "ycat")
                self.norm(ofox, ycat, HC, T, HC, gc[("fox_out_norm", l)], HC, "ycat", drows=cfg.WS)
                self.gemm(ycat, D, [Wg("w_out", l)], D, T, 1024, self.epi_resid(hT, hT, "hT", 1.0))
                if reached("mixout", l):
                    stopped = True
                    break
                XC = cfg.XDH // 128
                self.norm(hT, hn, DC, T, DC, gc[("xattn_norm", l)], DC, "hn")
                self.gemm(hn, D, [Wg("xattn_wq", l)], D, T, 1024, self.epi_store(qx, "qx"))
                self.norm(qx, qxn, DC, T, XC, gc[("xattn_q_norm", l)], XC, "qxn")
                self.norm(memT, memn, DC, cfg.MEM, DC, gc[("mem_norm", l)], DC, "memn")
                self.gemm(memn, D, [Wg("xattn_wk", l)], D, cfg.MEM, 1024, self.epi_store(kx, "kx"))
                self.norm(kx, kxn, DC, cfg.MEM, XC, gc[("xattn_k_norm", l)], XC, "kxn")
                self.gemm(memn, D, [Wg("xattn_wv", l)], D, cfg.MEM, 1024, self.epi_store(vx, "vx"))
                self.xattn(qxn, kxn, vx, ox)
                self.gemm(ox, D, [Wg("xattn_wo", l)], D, T, 1024, self.epi_resid(hT, hT, "hT", 1.0))
                if reached("xattn", l):
                    stopped = True
                    break
                self.norm(hT, hn, DC, T, DC, gc[("ffn2_norm", l)], DC, "hn")
                self.gemm(hn, D, [Wg("ffn2_w_gate", l), Wg("ffn2_w_up", l)], cfg.DFF, T, 1024, self.epi_swiglu(act, "act"))
                self.gemm(act, cfg.DFF, [Wg("ffn2_w_down", l)], D, T, 512, self.epi_resid(hT, hT, "hT", 0.5))
                dst = outT if last else hT
                self.norm(hT, dst, DC, T, DC, gc[("final_norm", l)], DC, "outT" if last else "hT")
            if stopped and not (skip_to == "xattn" and stop_after in ("xq0", "xkv0", "xcore0")):
                with self.phase() as pes:
                    cp = Pool(self, pes, "cp", [128, 512], F32, 4)
                    TW = min(512, T)
                    for c in range(DC):
                        for t0 in range(0, T, TW):
                            ct, cb = cp.get()
                            self.fw.dma("sync", lambda e, ct=ct, c=c, t0=t0: e.dma_start(
                                out=ct[:, 0:TW], in_=hT[c * 128:(c + 1) * 128, t0:t0 + TW]), write=cb)
                            self.fw.dma("sync", lambda e, ct=ct, c=c, t0=t0: e.dma_start(
                                out=outT[c * 128:(c + 1) * 128, t0:t0 + TW], in_=ct[:, 0:TW]), reads=[cb], dram_out="outT")
        return nc


WEIGHT_NAMES = ["ffn1_w_gate", "ffn1_w_up", "ffn1_w_down", "w_in", "ssm_w_glu", "w_out", "xattn_wq", "xattn_wk",
                "xattn_wv", "xattn_wo", "ffn2_w_gate", "ffn2_w_up", "ffn2_w_down"]


def shared_inputs(cfg, inp):
    m = {}
    L, G, N, P = cfg.depth, cfg.G, cfg.N, cfg.P
    NP = G // 2
    f32 = np.float32

    def state_layout(a):
        return np.ascontiguousarray(np.asarray(a, f32).reshape(L, NP, 2, N).transpose(0, 2, 3, 1).reshape(L, 128, NP))
    m["s5_are"] = state_layout(inp["ssm_A_re"])
    m["s5_aim"] = state_layout(inp["ssm_A_im"])
    m["s5_lst"] = state_layout(np.broadcast_to(np.asarray(inp["ssm_log_step"], f32)[:, :, None], (L, G, N)))
    for nm, key in (("s5_bre", "ssm_B_re"), ("s5_bim", "ssm_B_im")):
        B = np.asarray(inp[key], f32).reshape(L, NP, 2, N, P)
        blk = np.zeros((L, 2, P, NP, 2, N), f32)
        for g2 in range(2):
            blk[:, g2, :, :, g2, :] = B[:, :, g2].transpose(0, 3, 1, 2)
        m[nm] = np.ascontiguousarray(blk.reshape(L, 32, NP * 128))
    for nm, key in (("s5_cre", "ssm_C_re"), ("s5_cim", "ssm_C_im")):
        C = np.asarray(inp[key], f32).reshape(L, NP, 2, P, N)
        blk = np.zeros((L, 2, N, NP, 2, P), f32)
        for g2 in range(2):
            blk[:, g2, :, :, g2, :] = C[:, :, g2].transpose(0, 3, 1, 2)
        m[nm] = np.ascontiguousarray(blk.reshape(L, 128, NP * 32))
    m["s5_d"] = np.ascontiguousarray(np.asarray(inp["ssm_D"], f32).reshape(L, NP, 32).transpose(0, 2, 1))
    m["fox_b_f"] = np.ascontiguousarray(inp["fox_b_f"], dtype=f32)
    for n in GAIN_NAMES:
        m[n] = np.ascontiguousarray(inp[n], dtype=np.float32)
    for n in WEIGHT_NAMES:
        for l in range(cfg.depth):
            m[f"{n}_{l}"] = np.ascontiguousarray(inp[n][l], dtype=np.float32)
    return m


def core_inputs(cfg, inp, b, shared=None):
    m = dict(shared if shared is not None else shared_inputs(cfg, inp))
    m["xT"] = np.ascontiguousarray(np.asarray(inp["x"][b], dtype=np.float32).T)
    m["memT"] = np.ascontiguousarray(np.asarray(inp["mem"][b], dtype=np.float32).T)
    return m


_CACHE = {}

LAYERED = set(GAIN_NAMES) | set(WEIGHT_NAMES) | {"ssm_A_re", "ssm_A_im", "ssm_log_step", "ssm_B_re", "ssm_B_im",
                                                  "ssm_C_re", "ssm_C_im", "ssm_D", "fox_b_f"}


def kernel(**inputs):
    inp = {k: np.asarray(v) for k, v in inputs.items()}
    B, S, D = inp["x"].shape
    depth = inp["ffn1_norm"].shape[0]
    cfg = Cfg(D=D, T=S, depth=1, mem=inp["mem"].shape[1])
    if "nc" not in _CACHE:
        _CACHE["nc"] = MK(cfg).build()
    nc = _CACHE["nc"]
    cur = [np.ascontiguousarray(np.asarray(inp["x"][b], dtype=np.float32).T) for b in range(B)]
    memT = [np.ascontiguousarray(np.asarray(inp["mem"][b], dtype=np.float32).T) for b in range(B)]
    for l in range(depth):
        inp_l = {k: (v[l:l + 1] if k in LAYERED else v) for k, v in inp.items()}
        shared = shared_inputs(cfg, inp_l)
        in_maps = []
        for b in range(B):
            m = dict(shared)
            m["xT"] = cur[b]
            m["memT"] = memT[b]
            in_maps.append(m)
        res = run_bass_kernel_spmd(nc, in_maps, core_ids=list(range(B)))
        cur = [np.ascontiguousarray(res.results[b]["outT"]) for b in range(B)]
        del res, in_maps, shared
    out = np.stack([np.ascontiguousarray(cur[b].T) for b in range(B)], axis=0)
    return out.astype(np.float32)
```
